# Optimizing a Trainium2 kernel written in Bass

```python
import math
import jax, jax.numpy as jnp
from jax import lax
import numpy as np

D_MODEL = 1024
BATCH = 4
SEQ = 4096
DEPTH = 2

CHUNK = 64
HEAD_DIM = 64
MIX_WIDTH = D_MODEL
N_HEADS_TOTAL = MIX_WIDTH // HEAD_DIM
SWA_HEADS = N_HEADS_TOTAL // 2
SWA_KV_HEADS = SWA_HEADS // 4
FOX_HEADS = N_HEADS_TOTAL // 4
MLSTM_HEADS = N_HEADS_TOTAL - SWA_HEADS - FOX_HEADS
SWA_Q = SWA_HEADS * HEAD_DIM
SWA_KV = SWA_KV_HEADS * HEAD_DIM
FOX_W = FOX_HEADS * HEAD_DIM
MLSTM_W = MLSTM_HEADS * HEAD_DIM
IN_WIDTH = SWA_Q + 2 * SWA_KV + 3 * FOX_W + FOX_HEADS + 4 * MLSTM_W + 2 * MLSTM_HEADS
WINDOW = 128
WIN_CHUNKS = WINDOW // CHUNK
QBLK = 128
D_FF = 4 * D_MODEL
ROPE_THETA = 10000.0
EPS = 1e-6

kernel_name = "hybrid_swa_fox_mlstm_parallel_heads"


def _rms(x, gain):
    xf = x.astype(jnp.float32)
    y = xf * lax.rsqrt(jnp.mean(xf * xf, axis=-1, keepdims=True) + EPS)
    return (y * gain.astype(jnp.float32)).astype(x.dtype)


def _rope(x, pos):
    half = HEAD_DIM // 2
    inv = ROPE_THETA ** (-jnp.arange(half, dtype=jnp.float32) / half)
    ang = pos.astype(jnp.float32)[:, None] * inv[None, :]
    cos = jnp.cos(ang)[None, :, None, :]
    sin = jnp.sin(ang)[None, :, None, :]
    xf = x.astype(jnp.float32)
    x1, x2 = xf[..., :half], xf[..., half:]
    return jnp.concatenate([x1 * cos - x2 * sin, x2 * cos + x1 * sin], axis=-1).astype(x.dtype)


def _split_in_proj(z):
    sizes = (SWA_Q, SWA_KV, SWA_KV, FOX_W, FOX_W, FOX_W, FOX_HEADS,
             MLSTM_W, MLSTM_W, MLSTM_W, MLSTM_HEADS, MLSTM_HEADS, MLSTM_W)
    idx = np.cumsum(sizes)[:-1].tolist()
    return jnp.split(z, idx, axis=-1)


def _swa_attention(q, k, v, sinks):
    B, S = q.shape[0], q.shape[1]
    nb = S // QBLK
    G = SWA_HEADS // SWA_KV_HEADS
    qb = q.reshape(B, nb, QBLK, SWA_KV_HEADS, G, HEAD_DIM)

    def band(t):
        tb = t.reshape(B, nb, QBLK, SWA_KV_HEADS, HEAD_DIM)
        prev = jnp.concatenate([jnp.zeros_like(tb[:, :1]), tb[:, :-1]], axis=1)
        return jnp.concatenate([prev, tb], axis=2)

    kb, vb = band(k), band(v)
    s = jnp.einsum('bnqhgd,bnkhd->bnhgqk', qb, kb).astype(jnp.float32) / math.sqrt(HEAD_DIM)
    qpos = jnp.arange(nb)[:, None] * QBLK + jnp.arange(QBLK)[None, :]
    kpos = jnp.arange(nb)[:, None] * QBLK - QBLK + jnp.arange(2 * QBLK)[None, :]
    qc = (qpos // CHUNK)[:, :, None]
    kc = (kpos // CHUNK)[:, None, :]
    allowed = (kpos[:, None, :] >= 0) & (kc <= qc) & (kc >= qc - WIN_CHUNKS)
    s = jnp.where(allowed[None, :, None, None], s, -jnp.inf)
    sink = sinks.astype(jnp.float32).reshape(SWA_KV_HEADS, G)[None, None, :, :, None, None]
    m = jnp.maximum(jnp.max(s, axis=-1, keepdims=True), sink)
    p = jnp.exp(s - m)
    p = p / (jnp.sum(p, axis=-1, keepdims=True) + jnp.exp(sink - m))
    o = jnp.einsum('bnhgqk,bnkhd->bnqhgd', p.astype(v.dtype), vb)
    return o.reshape(B, S, SWA_Q)


def _fox_attention(q, k, v, log_f):
    B, S, H, D = q.shape
    nb = S // QBLK
    FT = jnp.cumsum(log_f, axis=1).transpose(0, 2, 1)
    qb = q.reshape(B, nb, QBLK, H, D).transpose(1, 0, 2, 3, 4)
    Fb = FT.reshape(B, H, nb, QBLK).transpose(2, 0, 1, 3)
    kpos = jnp.arange(S)

    def block(args):
        qi, Fi, n = args
        s = jnp.einsum('bqhd,bkhd->bhqk', qi, k).astype(jnp.float32) / math.sqrt(D)
        s = s + Fi[..., None] - FT[:, :, None, :]
        qpos = n * QBLK + jnp.arange(QBLK)
        s = jnp.where((kpos[None, :] <= qpos[:, None])[None, None], s, -jnp.inf)
        p = jax.nn.softmax(s, axis=-1).astype(v.dtype)
        return jnp.einsum('bhqk,bkhd->bqhd', p, v)

    o = lax.map(block, (qb, Fb, jnp.arange(nb)))
    return o.transpose(1, 0, 2, 3, 4).reshape(B, S, H * D)


def _mlstm(q, k, v, i_pre, f_pre):
    B, S, H, D = q.shape
    nc, L = S // CHUNK, CHUNK

    def chunks(t):
        return t.astype(jnp.float32).reshape(B, nc, L, H, D).transpose(0, 3, 1, 2, 4)

    qc, kc, vc = chunks(q), chunks(k) / math.sqrt(D), chunks(v)
    ig = i_pre.reshape(B, nc, L, H).transpose(0, 3, 1, 2)
    lf = jax.nn.log_sigmoid(f_pre).reshape(B, nc, L, H).transpose(0, 3, 1, 2)
    b = jnp.cumsum(lf, axis=-1)
    g = b[..., -1]
    causal = jnp.tril(jnp.ones((L, L), dtype=bool))
    Dm = jnp.where(causal, b[..., :, None] - b[..., None, :] + ig[..., None, :], -jnp.inf)
    a = g[..., None] - b + ig
    a_max = jnp.max(a, axis=-1)
    wa = jnp.exp(a - a_max[..., None])
    C_loc = jnp.einsum('bhcl,bhclv,bhclk->bhcvk', wa, vc, kc)
    n_loc = jnp.einsum('bhcl,bhclk->bhck', wa, kc)

    def step(carry, inp):
        C, n, m = carry
        Cl, nl, gl, aml = inp
        m_new = jnp.maximum(gl + m, aml)
        s_old = jnp.exp(gl + m - m_new)
        s_loc = jnp.exp(aml - m_new)
        C_new = s_old[..., None, None] * C + s_loc[..., None, None] * Cl
        n_new = s_old[..., None] * n + s_loc[..., None] * nl
        return (C_new, n_new, m_new), (C, n, m)

    init = (jnp.zeros((B, H, D, D), jnp.float32), jnp.zeros((B, H, D), jnp.float32),
            jnp.zeros((B, H), jnp.float32))
    xs = (C_loc.transpose(2, 0, 1, 3, 4), n_loc.transpose(2, 0, 1, 3),
          g.transpose(2, 0, 1), a_max.transpose(2, 0, 1))
    _, (C_prev, n_prev, m_prev) = lax.scan(step, init, xs)
    C_prev = C_prev.transpose(1, 2, 0, 3, 4)
    n_prev = n_prev.transpose(1, 2, 0, 3)
    m_prev = m_prev.transpose(1, 2, 0)

    inter = b + m_prev[..., None]
    m_t = jnp.maximum(inter, jnp.max(Dm, axis=-1))
    w_inter = jnp.exp(inter - m_t)
    Sm = jnp.einsum('bhcld,bhcsd->bhcls', qc, kc) * jnp.exp(Dm - m_t[..., None])
    num = (w_inter[..., None] * jnp.einsum('bhcvk,bhclk->bhclv', C_prev, qc)
           + jnp.einsum('bhcls,bhcsv->bhclv', Sm, vc))
    den = w_inter * jnp.einsum('bhck,bhclk->bhcl', n_prev, qc) + jnp.sum(Sm, axis=-1)
    h = num / jnp.maximum(jnp.abs(den), jnp.exp(-m_t))[..., None]
    return h.transpose(0, 2, 3, 1, 4).reshape(B, S, H, D)


def _layer(x, norm1, w_in, swa_q_norm, swa_k_norm, swa_sinks, fox_q_norm, fox_k_norm,
           fox_f_bias, mlstm_i_bias, mlstm_f_bias, mlstm_out_norm, w_out, norm2, w_ff1, w_ff2):
    B, S, _ = x.shape
    h = _rms(x, norm1)
    z = h @ w_in
    (aq, ak, av, bq, bk, bv, bf, cq, ck, cv, ci, cf, co) = _split_in_proj(z)
    pos = jnp.arange(S)

    aq = _rope(_rms(aq.reshape(B, S, SWA_HEADS, HEAD_DIM), swa_q_norm), pos)
    ak = _rope(_rms(ak.reshape(B, S, SWA_KV_HEADS, HEAD_DIM), swa_k_norm), pos)
    ya = _swa_attention(aq, ak, av.reshape(B, S, SWA_KV_HEADS, HEAD_DIM), swa_sinks)

    bq = _rms(bq.reshape(B, S, FOX_HEADS, HEAD_DIM), fox_q_norm)
    bk = _rms(bk.reshape(B, S, FOX_HEADS, HEAD_DIM), fox_k_norm)
    log_f = jax.nn.log_sigmoid(bf.astype(jnp.float32) + fox_f_bias.astype(jnp.float32))
    yb = _fox_attention(bq, bk, bv.reshape(B, S, FOX_HEADS, HEAD_DIM), log_f)

    hc = _mlstm(cq.reshape(B, S, MLSTM_HEADS, HEAD_DIM), ck.reshape(B, S, MLSTM_HEADS, HEAD_DIM),
                cv.reshape(B, S, MLSTM_HEADS, HEAD_DIM),
                ci.astype(jnp.float32) + mlstm_i_bias.astype(jnp.float32),
                cf.astype(jnp.float32) + mlstm_f_bias.astype(jnp.float32))
    hc = _rms(hc, mlstm_out_norm)
    og = jax.nn.sigmoid(co.astype(jnp.float32)).reshape(B, S, MLSTM_HEADS, HEAD_DIM)
    yc = (og * hc).reshape(B, S, MLSTM_W).astype(x.dtype)

    x = x + jnp.concatenate([ya, yb, yc], axis=-1) @ w_out
    u = _rms(x, norm2) @ w_ff1
    return x + jnp.square(jax.nn.relu(u)) @ w_ff2


def setup_inputs(seed: int = 0) -> dict:
    key = jax.random.key(seed)
    ks = jax.random.split(key, 17)
    f32 = jnp.float32

    def gain(k, shape):
        return 1.0 + 0.02 * jax.random.normal(k, shape, f32)

    return {
        "x": jax.random.normal(ks[0], (BATCH, SEQ, D_MODEL), f32),
        "norm1": gain(ks[1], (DEPTH, D_MODEL)),
        "w_in": jax.random.normal(ks[2], (DEPTH, D_MODEL, IN_WIDTH), f32) * D_MODEL ** -0.5,
        "swa_q_norm": gain(ks[3], (DEPTH, HEAD_DIM)),
        "swa_k_norm": gain(ks[4], (DEPTH, HEAD_DIM)),
        "swa_sinks": jax.random.normal(ks[5], (DEPTH, SWA_HEADS), f32),
        "fox_q_norm": gain(ks[6], (DEPTH, HEAD_DIM)),
        "fox_k_norm": gain(ks[7], (DEPTH, HEAD_DIM)),
        "fox_f_bias": jax.random.uniform(ks[8], (DEPTH, FOX_HEADS), f32, 2.0, 6.0),
        "mlstm_i_bias": 0.1 * jax.random.normal(ks[9], (DEPTH, MLSTM_HEADS), f32),
        "mlstm_f_bias": jax.random.uniform(ks[10], (DEPTH, MLSTM_HEADS), f32, 3.0, 6.0),
        "mlstm_out_norm": gain(ks[11], (DEPTH, MLSTM_HEADS, HEAD_DIM)),
        "w_out": jax.random.normal(ks[12], (DEPTH, MIX_WIDTH, D_MODEL), f32) * MIX_WIDTH ** -0.5,
        "norm2": gain(ks[13], (DEPTH, D_MODEL)),
        "w_ff1": jax.random.normal(ks[14], (DEPTH, D_MODEL, D_FF), f32) * D_MODEL ** -0.5,
        "w_ff2": jax.random.normal(ks[15], (DEPTH, D_FF, D_MODEL), f32) * D_FF ** -0.5,
    }


def reference(x, norm1, w_in, swa_q_norm, swa_k_norm, swa_sinks, fox_q_norm, fox_k_norm,
              fox_f_bias, mlstm_i_bias, mlstm_f_bias, mlstm_out_norm, w_out, norm2, w_ff1, w_ff2):
    for l in range(DEPTH):
        x = _layer(x, norm1[l], w_in[l], swa_q_norm[l], swa_k_norm[l], swa_sinks[l],
                   fox_q_norm[l], fox_k_norm[l], fox_f_bias[l], mlstm_i_bias[l], mlstm_f_bias[l],
                   mlstm_out_norm[l], w_out[l], norm2[l], w_ff1[l], w_ff2[l])
    return x
```

```python
import contextlib
import numpy as np
import ml_dtypes
import concourse.bass as bass
import concourse.mybir as mybir
from concourse.bass_utils import run_bass_kernel_spmd

F32 = mybir.dt.float32
BF16 = mybir.dt.bfloat16
AF = mybir.ActivationFunctionType
ALU = mybir.AluOpType
AX = mybir.AxisListType

S = 4096
D = 1024
DFF = 4096
EPS = 1e-6
NT = S // 128
HT = 2048


class Res:
    __slots__ = ("name", "w", "r")

    def __init__(self, name):
        self.name = name
        self.w = None
        self.r = []


class Prog:
    ENGS = ("pe", "act", "dve", "pool", "sp")

    def __init__(self, nc):
        self.nc = nc
        self.ops = {e: [] for e in self.ENGS}
        self.cnt = {e: 0 for e in self.ENGS}
        self.known = {e: {} for e in self.ENGS}
        self.dma_cnt = {}
        self.final_events = []

    def _need(self, eng, ev, waits):
        if ev is None:
            return
        key, val, src = ev
        if self.known[eng].get(key, 0) >= val:
            return
        self.known[eng][key] = val
        waits[key] = max(waits.get(key, 0), val)

    def _deps(self, eng, reads, writes):
        waits = {}
        for r in reads:
            if r.w is not None and not (r.w[2] == eng and eng in ("pe", "sp")):
                self._need(eng, r.w, waits)
        for w in writes:
            if w.w is not None and w.w[2] != eng:
                self._need(eng, w.w, waits)
            for ev in w.r:
                if ev[2] != eng:
                    self._need(eng, ev, waits)
        return list(waits.items())

    def _commit(self, ev, reads, writes):
        for r in reads:
            r.r.append(ev)
        for w in writes:
            w.w = ev
            w.r = []

    def op(self, eng, fn, reads=(), writes=()):
        waits = self._deps(eng, reads, writes)
        self.cnt[eng] += 1
        ev = (eng, self.cnt[eng], eng)
        self.ops[eng].append((waits, fn, (eng, 1)))
        self._commit(ev, reads, writes)
        return ev

    def dma(self, queue, fn, semkey, reads=(), writes=(), final=False):
        waits = self._deps(queue, reads, writes)
        n = self.dma_cnt.get(semkey, 0) + 1
        self.dma_cnt[semkey] = n
        ev = (semkey, 16 * n, "dma")
        self.ops[queue].append((waits, fn, (semkey, 16)))
        self._commit(ev, reads, writes)
        if final:
            self.final_events.append(ev)
        return ev

    def emit(self):
        nc = self.nc
        keys = list(self.ENGS) + list(self.dma_cnt.keys())
        with contextlib.ExitStack() as st:
            sems = {k: st.enter_context(nc.semaphore("s_" + k)) for k in keys}
            block = st.enter_context(nc.Block())
            fin = {}
            for key, val, _ in self.final_events:
                fin[key] = max(fin.get(key, 0), val)

            def make(engname):
                def body(e):
                    for waits, fn, (ik, iv) in self.ops[engname]:
                        for k, v in waits:
                            e.wait_ge(sems[k], v)
                        ins = fn(e)
                        ins.then_inc(sems[ik], iv)
                    if engname == "sp":
                        for k, v in fin.items():
                            e.wait_ge(sems[k], v)
                return body

            block.tensor(make("pe"))
            block.scalar(make("act"))
            block.vector(make("dve"))
            block.gpsimd(make("pool"))
            block.sync(make("sp"))


class Ctx:
    def __init__(self, arena_cols=51 * 1024):
        self.nc = bass.Bass("TRN2", target_bir_lowering=False)
        self.P = Prog(self.nc)
        self.st = contextlib.ExitStack()
        self.arena = self.st.enter_context(self.nc.sbuf_tensor("arena", [128, arena_cols], F32))
        self.arena_cols = arena_cols
        self.off = 0
        self.banks = []
        self.bres = []
        for i in range(4):
            self.banks.append(self.st.enter_context(self.nc.psum_tensor("pb%d" % i, [128, 1024], F32)))
            self.bres.append((Res("pb%da" % i), Res("pb%db" % i)))

    def alloc(self, shape, dtype=F32, name=None):
        n = int(np.prod(shape))
        cols = n if dtype == F32 else (n + 1) // 2
        assert self.off + cols <= self.arena_cols, ("SBUF arena overflow", name, self.off, cols)
        ap = self.arena[:, self.off:self.off + cols]
        self.off += cols
        if dtype != F32:
            ap = ap.bitcast(dtype)
            if n % 2:
                ap = ap[:, 0:n]
        if len(shape) == 2:
            ap = ap.rearrange("p (a b) -> p a b", a=shape[0])
        elif len(shape) == 3:
            ap = ap.rearrange("p (a b c) -> p a b c", a=shape[0], b=shape[1])
        elif len(shape) == 4:
            ap = ap.rearrange("p (a b c d) -> p a b c d", a=shape[0], b=shape[1], c=shape[2])
        return ap

    def bank(self, i, half):
        return self.banks[i][:, half * 512:(half + 1) * 512], self.bres[i][half]

    def bank_bf(self, i, half):
        return self.banks[i][:, half * 512:(half + 1) * 512].bitcast(BF16), self.bres[i][half]

    def dram_in(self, name, shape, dtype=F32):
        return self.nc.dram_tensor(name, list(shape), dtype, kind="ExternalInput").ap()

    def dram_out(self, name, shape, dtype=F32):
        return self.nc.dram_tensor(name, list(shape), dtype, kind="ExternalOutput").ap()

    def finish(self):
        self.P.emit()
        self.st.close()
        return self.nc


def emit_norm_T(cx, xt, xres, gain_bc, gres, hT, hTres, t, ident, identres, scratch, scrres, hb, hbres,
                small, smallres, tp, tpres):
    P = cx.P
    ssq = small[:, 0:1]
    rstd = small[:, 1:2]
    P.op("act", lambda e: e.activation(out=scratch, in_=xt, func=AF.Square), reads=[xres], writes=[scrres])
    P.op("dve", lambda e: e.tensor_reduce(out=ssq, in_=scratch, axis=AX.X, op=ALU.add), reads=[scrres], writes=[smallres])
    P.op("dve", lambda e: e.tensor_scalar(out=rstd, in0=ssq, scalar1=1.0 / D, scalar2=EPS, op0=ALU.mult, op1=ALU.add),
         reads=[smallres], writes=[smallres])
    P.op("act", lambda e: e.activation(out=rstd, in_=rstd, func=AF.Ln), reads=[smallres], writes=[smallres])
    P.op("act", lambda e: e.activation(out=rstd, in_=rstd, func=AF.Exp, scale=-0.5), reads=[smallres], writes=[smallres])
    P.op("dve", lambda e: e.scalar_tensor_tensor(out=hb, in0=xt, scalar=rstd, in1=gain_bc, op0=ALU.mult, op1=ALU.mult),
         reads=[xres, smallres, gres], writes=[hbres])
    for kc in range(8):
        P.op("pe", lambda e, kc=kc: e.transpose(tp[:, kc * 128:(kc + 1) * 128], hb[:, kc * 128:(kc + 1) * 128], ident),
             reads=[hbres, identres], writes=[tpres])
    P.op("act", lambda e: e.activation(out=hT[:, :, t * 128:(t + 1) * 128],
                                       in_=tp[:, 0:1024].rearrange("p (k n) -> p k n", k=8), func=AF.Copy),
         reads=[tpres], writes=[hTres])


def build_p0():
    cx = Ctx()
    P = cx.P
    x_d = cx.dram_in("x", [HT, D])
    g_d = cx.dram_in("g", [128, D])
    cb_d = cx.dram_in("cb", [128, 256], BF16)
    hT_d = cx.dram_out("hT", [D, HT], BF16)
    gain = cx.alloc([D]); gres = Res("gain")
    cb = cx.alloc([256], BF16); cbres = Res("cb")
    ident = cb[:, 0:128]
    hT = cx.alloc([8, HT], BF16); hTres = [Res("hT%d" % t) for t in range(16)]
    xts = [cx.alloc([D]) for _ in range(2)]; xres = [Res("x0"), Res("x1")]
    scratch = cx.alloc([D]); scrres = Res("scr")
    hbs = [cx.alloc([D], BF16) for _ in range(2)]; hbres = [Res("hb0"), Res("hb1")]
    smalls = [cx.alloc([8]) for _ in range(2)]; smres = [Res("sm0"), Res("sm1")]
    P.dma("sp", lambda e: e.dma_start(out=gain, in_=g_d), "ld_g", writes=[gres])
    P.dma("sp", lambda e: e.dma_start(out=cb, in_=cb_d), "ld_cb", writes=[cbres])
    for t in range(16):
        s = t % 2
        P.dma("sp", lambda e, t=t, s=s: e.dma_start(out=xts[s], in_=x_d[t * 128:(t + 1) * 128, :]), "ld_x%d" % s,
              writes=[xres[s]])
        tp, tpres = cx.bank_bf(s, 0)
        emit_norm_T(cx, xts[s], xres[s], gain, gres, hT, hTres[t], t, ident, cbres, scratch, scrres,
                    hbs[s], hbres[s], smalls[s], smres[s], tp, tpres)
    for kc in range(8):
        P.dma("sp", lambda e, kc=kc: e.dma_start(out=hT_d[kc * 128:(kc + 1) * 128, :], in_=hT[:, kc, :]), "st_h",
              reads=hTres, final=True)
    return cx.finish()


def build_f(last):
    cx = Ctx()
    P = cx.P
    x_d = cx.dram_in("x", [HT, D])
    yT_d = cx.dram_in("yT", [D, HT], BF16)
    wo_d = cx.dram_in("w_out", [D, D])
    w1_d = cx.dram_in("w_ff1", [D, DFF])
    w2_d = cx.dram_in("w_ff2", [DFF, D])
    g2_d = cx.dram_in("g2", [128, D])
    cb_d = cx.dram_in("cb", [128, 256], BF16)
    xo_d = cx.dram_out("xo", [HT, D])
    if not last:
        gn_d = cx.dram_in("gn", [128, D])
        hT_d = cx.dram_out("hT", [D, HT], BF16)

    x = cx.alloc([16, D]); xres = [Res("x%d" % t) for t in range(16)]
    h2T = cx.alloc([8, HT], BF16); h2res = [Res("h2T%d" % t) for t in range(16)]
    gain2 = cx.alloc([D]); g2res = Res("g2")
    cb = cx.alloc([256], BF16); cbres = Res("cb")
    ident = cb[:, 0:128]
    scratch = cx.alloc([D]); scrres = Res("scr")
    hbs = [cx.alloc([D], BF16) for _ in range(2)]; hbres = [Res("hb0"), Res("hb1")]
    smalls = [cx.alloc([8]) for _ in range(2)]; smres = [Res("sm0"), Res("sm1")]
    wo = cx.alloc([8, D], BF16); wores = Res("wo")
    yTs = [cx.alloc([8, 512], BF16) for _ in range(2)]; yres = [Res("yT0"), Res("yT1")]
    w1s = [cx.alloc([8, 512], BF16) for _ in range(2)]; w1res = [Res("w1a"), Res("w1b")]
    w2s = [cx.alloc([4, D], BF16) for _ in range(2)]; w2res = [Res("w2a"), Res("w2b")]
    uTs = [cx.alloc([4, 512], BF16) for _ in range(2)]; ures = [Res("u0"), Res("u1")]
    rls = [cx.alloc([512], BF16) for _ in range(2)]; rlres = [Res("rl0"), Res("rl1")]

    P.dma("sp", lambda e: e.dma_start(out=gain2, in_=g2_d), "ld_g2", writes=[g2res])
    P.dma("sp", lambda e: e.dma_start(out=cb, in_=cb_d), "ld_cb", writes=[cbres])
    for kc in range(8):
        P.dma("pool", lambda e, kc=kc: e.dma_start(out=wo[:, kc, :], in_=wo_d[kc * 128:(kc + 1) * 128, :]), "ld_wo",
              writes=[wores])
    for t in range(16):
        P.dma("sp", lambda e, t=t: e.dma_start(out=x[:, t, :], in_=x_d[t * 128:(t + 1) * 128, :]), "ld_x%d" % t,
              writes=[xres[t]])

    for t in range(16):
        ys = (t // 4) % 2
        yT = yTs[ys]
        if t % 4 == 0:
            for kc in range(8):
                P.dma("sp", lambda e, kc=kc, t=t, yT=yT: e.dma_start(
                    out=yT[:, kc, :], in_=yT_d[kc * 128:(kc + 1) * 128, t * 128:t * 128 + 512]), "ld_y%d" % ys,
                    writes=[yres[ys]])
        for half in range(2):
            acc, accres = cx.bank(t % 2, half)
            for kc in range(8):
                P.op("pe", lambda e, kc=kc, t=t, half=half, acc=acc, yT=yT: e.matmul(
                    acc, lhsT=yT[:, kc, (t % 4) * 128:(t % 4 + 1) * 128], rhs=wo[:, kc, half * 512:(half + 1) * 512],
                    start=(kc == 0), stop=(kc == 7)), reads=[yres[ys], wores], writes=[accres])
            P.op("dve", lambda e, t=t, half=half, acc=acc: e.tensor_tensor(
                out=x[:, t, half * 512:(half + 1) * 512], in0=acc, in1=x[:, t, half * 512:(half + 1) * 512], op=ALU.add),
                reads=[accres, xres[t]], writes=[xres[t]])
        s = t % 2
        tp, tpres = cx.bank_bf(2 + s, 0)
        emit_norm_T(cx, x[:, t, :], xres[t], gain2, g2res, h2T, h2res[t], t, ident, cbres, scratch, scrres,
                    hbs[s], hbres[s], smalls[s], smres[s], tp, tpres)

    for j in range(8):
        s = j % 2
        w1, w2 = w1s[s], w2s[s]
        for kc in range(8):
            P.dma("pool", lambda e, kc=kc, j=j, w1=w1: e.dma_start(
                out=w1[:, kc, :], in_=w1_d[kc * 128:(kc + 1) * 128, j * 512:(j + 1) * 512]), "ld_w1%d" % s,
                writes=[w1res[s]])
        for mc in range(4):
            P.dma("pool", lambda e, mc=mc, j=j, w2=w2: e.dma_start(
                out=w2[:, mc, :], in_=w2_d[j * 512 + mc * 128:j * 512 + (mc + 1) * 128, :]), "ld_w2%d" % s,
                writes=[w2res[s]])
        for n in range(4):
            us = (j * 4 + n) % 2
            uT = uTs[us]
            for mc in range(4):
                pb, pbres = cx.bank(2 + (mc % 2), 1)
                for kc in range(8):
                    P.op("pe", lambda e, kc=kc, mc=mc, n=n, pb=pb, w1=w1: e.matmul(
                        pb, lhsT=w1[:, kc, mc * 128:(mc + 1) * 128], rhs=h2T[:, kc, n * 512:(n + 1) * 512],
                        start=(kc == 0), stop=(kc == 7)),
                        reads=[w1res[s]] + h2res[n * 4:(n + 1) * 4], writes=[pbres])
                rl = rls[mc % 2]
                P.op("act", lambda e, pb=pb, rl=rl: e.activation(out=rl, in_=pb, func=AF.Relu),
                     reads=[pbres], writes=[rlres[mc % 2]])
                P.op("pool", lambda e, rl=rl, uT=uT, mc=mc: e.tensor_tensor(out=uT[:, mc, :], in0=rl, in1=rl, op=ALU.mult),
                     reads=[rlres[mc % 2]], writes=[ures[us]])
            for tt in range(4):
                t = n * 4 + tt
                for half in range(2):
                    acc, accres = cx.bank(tt % 2, half)
                    for mc in range(4):
                        P.op("pe", lambda e, mc=mc, tt=tt, half=half, acc=acc, uT=uT, w2=w2: e.matmul(
                            acc, lhsT=uT[:, mc, tt * 128:(tt + 1) * 128], rhs=w2[:, mc, half * 512:(half + 1) * 512],
                            start=(mc == 0), stop=(mc == 3)), reads=[ures[us], w2res[s]], writes=[accres])
                    P.op("dve", lambda e, t=t, half=half, acc=acc: e.tensor_tensor(
                        out=x[:, t, half * 512:(half + 1) * 512], in0=acc, in1=x[:, t, half * 512:(half + 1) * 512],
                        op=ALU.add), reads=[accres, xres[t]], writes=[xres[t]])

    for t in range(16):
        P.dma("sp", lambda e, t=t: e.dma_start(out=xo_d[t * 128:(t + 1) * 128, :], in_=x[:, t, :]), "st_x",
              reads=[xres[t]], final=True)
    if not last:
        gainn = gain2
        P.dma("sp", lambda e: e.dma_start(out=gainn, in_=gn_d), "ld_g2", reads=[g2res], writes=[g2res])
        for t in range(16):
            s = t % 2
            tp, tpres = cx.bank_bf(2 + s, 0)
            emit_norm_T(cx, x[:, t, :], xres[t], gainn, g2res, h2T, h2res[t], t, ident, cbres, scratch, scrres,
                        hbs[s], hbres[s], smalls[s], smres[s], tp, tpres)
        for kc in range(8):
            P.dma("sp", lambda e, kc=kc: e.dma_start(out=hT_d[kc * 128:(kc + 1) * 128, :], in_=h2T[:, kc, :]), "st_h",
                  reads=h2res, final=True)
    return cx.finish()


def consts_bf():
    cb = np.zeros((128, 256), np.float32)
    cb[:, 0:128] = np.eye(128)
    cb[:, 128:256] = np.triu(np.ones((128, 128)))
    return cb.astype(ml_dtypes.bfloat16)


def run(nc, in_maps):
    res = run_bass_kernel_spmd(nc, in_maps, core_ids=list(range(8)))
    return res.results


def bc(ap, shape):
    return ap.broadcast_to(list(shape))


DBG_NJ = 8
DBG_STAGE = 9


def build_m():
    cx = Ctx()
    P = cx.P
    hT_d = cx.dram_in("hT", [D, S], BF16)
    wqk_d = cx.dram_in("wqk", [D, 640])
    wv_d = cx.dram_in("wv", [D, 576])
    wc_d = cx.dram_in("wc", [D, 256])
    wg_d = cx.dram_in("wg", [D, 128])
    gqk_d = cx.dram_in("gqk", [128, 640])
    gc_d = cx.dram_in("gc", [128, 128])
    sk_d = cx.dram_in("sk", [128, 4])
    pv_d = cx.dram_in("pv", [2, 4])
    cf_d = cx.dram_in("cf", [128, 2304])
    c2_d = cx.dram_in("c2", [2, 1920])
    cb_d = cx.dram_in("cb", [128, 256], BF16)
    yT_d = cx.dram_out("yT", [512, S], BF16)

    A = cx.alloc
    wqk = A([8, 640], BF16); wv = A([8, 576], BF16); wc = A([8, 256], BF16); wg = A([8, 128], BF16)
    wres = Res("w")
    gqk = A([640]); gc = A([128]); esk = A([4]); pv = A([4]); npv = A([4])
    cf = A([2304]); c2 = A([1920]); cb = A([256], BF16)
    cres = Res("consts")
    identf = cf[:, 0:128]; m2 = cf[:, 128:256]
    cosT = cf[:, 256:1280].rearrange("p (t f) -> p t f", f=32)
    sinT = cf[:, 1280:2304].rearrange("p (t f) -> p t f", f=32)
    selB = c2[0:2, 0:128]; sel0 = c2[0:2, 128:256]; sel1 = c2[0:2, 256:384]
    rmask = c2[0:2, 384:896]; negbig = c2[0:2, 896:1408]; onesr = c2[0:2, 1408:1920]
    identb = cb[:, 0:128]; tri = cb[:, 128:256]
    bkT = A([S], BF16); bkres = [Res("bkT%d" % t) for t in range(NT)]
    bV = A([NT, 2, 65], BF16); bVres = [Res("bV%d" % t) for t in range(NT)]
    FnegCol = A([NT, 2]); fcres = Res("FnegCol")
    carryF = A([1]); mcarry = A([1]); carres = Res("carry")
    Cst = A([65]); Cbf = [A([65], BF16) for _ in range(2)]
    cstres = Res("Cst"); cbfres = [Res("Cbf0"), Res("Cbf1")]
    hts = [A([8, 512], BF16) for _ in range(2)]; htres = [Res("ht0"), Res("ht1")]
    aqT = A([2, 512], BF16); aqres = Res("aqT")
    akTw = A([640], BF16); akres = Res("akTw")
    aVw = A([5, 65], BF16); aVres = Res("aVw")
    bqT = A([512], BF16); bqres = Res("bqT")
    cqT = A([512], BF16); ckT = A([512], BF16); cqres = Res("cqT"); ckres = Res("ckT")
    cV1 = A([4, 2, 65], BF16); cVs = A([4, 2, 65], BF16); kw = A([4, 2, 64], BF16); og = A([4, 128])
    vpres = [Res("vpost%d" % t) for t in range(4)]
    rows = {k: A([512])[0:2, :] for k in
            "eb Lb ec Lc ig Bneg d a am inter cmx rmx mt t2 t3".split()}
    rres = {k: Res("row_" + k) for k in rows}
    stk = A([6, 512])[0:2]; stkres = Res("stk")
    smallr = A([64])[0:2, :]; smres = Res("smallr")
    amax = smallr[:, 0:8]; gpos = smallr[:, 8:16]; mnew = smallr[:, 16:24]; mprev = smallr[:, 24:32]
    so16 = smallr[:, 32:48]
    tokq = A([48]); tokqres = Res("tokq")
    tokq4 = tokq.rearrange("p (t q h) -> p t q h", t=4, q=6)
    sbc = A([18]); sbcres = Res("sbc")
    biasf = A([2, NT]); biasres = Res("biasf")
    sqb = A([640]); sqres = Res("sqb")
    ssq = A([16]); ssqres = Res("ssq")
    qn = A([640]); qnres = Res("qn")
    rA = A([384]); rB = A([384]); rres2 = Res("rope")
    qkb = A([640], BF16); qkbres = Res("qkb")
    PTa = A([1024], BF16); ptares = Res("PTa")
    PTf = [A([512], BF16) for _ in range(2)]; ptfres = [Res("PTf0"), Res("PTf1")]
    sm4 = A([16]); sm4res = Res("sm4")
    ytok = A([4, 512], BF16); ytres = [Res("ytok%d" % t) for t in range(4)]
    yTst = A([4, 512], BF16); ystres = Res("yTst")
    AT = [A([128], BF16) for _ in range(2)]; atres = [Res("AT0"), Res("AT1")]
    tmpc = A([65]); tmpcres = Res("tmpc")
    t1 = A([2, 65]); hn = A([2, 65]); hnres = Res("hn")
    ht_ = A([2, 64]); sqh = A([2, 64]); htres2 = Res("ht_")
    sm2 = A([16]); sm2res = Res("sm2")
    eo = A([128]); eores = Res("eo")

    for (dst, src, n) in ((wqk, wqk_d, 640), (wv, wv_d, 576), (wc, wc_d, 256), (wg, wg_d, 128)):
        for kc in range(8):
            P.dma("pool", lambda e, dst=dst, src=src, kc=kc: e.dma_start(
                out=dst[:, kc, :], in_=src[kc * 128:(kc + 1) * 128, :]), "ld_w", writes=[wres])
    for (dst, src) in ((gqk, gqk_d), (gc, gc_d), (esk, sk_d), (cf, cf_d), (cb, cb_d)):
        P.dma("sp", lambda e, dst=dst, src=src: e.dma_start(out=dst, in_=src), "ld_c", writes=[cres])
    P.dma("sp", lambda e: e.dma_start(out=c2[0:2, :], in_=c2_d), "ld_c", writes=[cres])
    P.dma("sp", lambda e: e.dma_start(out=pv[0:2, :], in_=pv_d), "ld_c", writes=[cres])
    P.op("dve", lambda e: e.tensor_scalar(out=npv[0:2, :], in0=pv[0:2, :], scalar1=-1.0, scalar2=None, op0=ALU.mult),
         reads=[cres], writes=[cres])
    P.op("act", lambda e: e.activation(out=esk, in_=esk, func=AF.Exp), reads=[cres], writes=[cres])
    P.op("pool", lambda e: e.memset(bV, 1.0), writes=bVres)
    P.op("pool", lambda e: e.memset(aVw, 1.0), writes=[aVres])
    P.op("pool", lambda e: e.memset(cV1, 1.0), writes=vpres)
    P.op("pool", lambda e: e.memset(Cst, 0.0), writes=[cstres])
    P.op("pool", lambda e: e.memset(Cbf[0], 0.0), writes=[cbfres[0]])
    P.op("pool", lambda e: e.memset(carryF, 0.0), writes=[carres])
    P.op("pool", lambda e: e.memset(mcarry, 0.0), writes=[carres])
    cbi = 0
    gcur = [0.0]
    realop = P.op

    def gated(eng, fn, reads=(), writes=()):
        if DBG_STAGE >= gcur[0]:
            return realop(eng, fn, reads, writes)
    P.op = gated

    def act_exp(out, in_, rd, wr, scale=1.0, bias=None):
        if bias is None:
            P.op("act", lambda e: e.activation(out=out, in_=in_, func=AF.Exp, scale=scale), reads=rd, writes=wr)
        else:
            P.op("act", lambda e: e.activation(out=out, in_=in_, func=AF.Exp, scale=scale, bias=bias), reads=rd, writes=wr)

    def rstd_chain(dst, src, n, rd, wr):
        P.op("dve", lambda e: e.tensor_scalar(out=dst, in0=src, scalar1=1.0 / 64, scalar2=EPS, op0=ALU.mult, op1=ALU.add),
             reads=rd, writes=wr)
        P.op("act", lambda e: e.activation(out=dst, in_=dst, func=AF.Ln), reads=wr, writes=wr)
        P.op("act", lambda e: e.activation(out=dst, in_=dst, func=AF.Exp, scale=-0.5), reads=wr, writes=wr)

    for j in range(DBG_NJ):
        hs = j % 2
        ht = hts[hs]
        for kc in range(8):
            P.dma("sp", lambda e, kc=kc, j=j, ht=ht: e.dma_start(
                out=ht[:, kc, :], in_=hT_d[kc * 128:(kc + 1) * 128, j * 512:(j + 1) * 512]), "ld_h%d" % hs,
                writes=[htres[hs]])

        gcur[0] = 0.0
        G, Gres = cx.bank(0, 0)
        for kc in range(8):
            P.op("pe", lambda e, kc=kc, ht=ht: e.matmul(G, lhsT=wg[:, kc, :], rhs=ht[:, kc, :], start=(kc == 0), stop=(kc == 7)),
                 reads=[wres, htres[hs]], writes=[Gres])
        R = rows
        act_exp(R["eb"], G[0:2, :], [Gres, cres], [rres["eb"]], scale=-1.0, bias=npv[0:2, 0:1])
        P.op("act", lambda e: e.activation(out=R["Lb"], in_=R["eb"], func=AF.Ln, bias=onesr[:, 0:1]),
             reads=[rres["eb"], cres], writes=[rres["Lb"]])
        act_exp(R["ec"], G[64:66, :], [Gres, cres], [rres["ec"]], scale=-1.0, bias=npv[0:2, 2:3])
        P.op("act", lambda e: e.activation(out=R["Lc"], in_=R["ec"], func=AF.Ln, bias=onesr[:, 0:1]),
             reads=[rres["ec"], cres], writes=[rres["Lc"]])
        P.op("act", lambda e: e.activation(out=R["ig"], in_=G[32:34, :], func=AF.Identity, bias=pv[0:2, 1:2]),
             reads=[Gres, cres], writes=[rres["ig"]])
        P.op("dve", lambda e: e.tensor_tensor_scan(out=stk[:, 0, :], data0=onesr, data1=R["Lb"], initial=carryF[0:2, 0:1],
                                                   op0=ALU.mult, op1=ALU.add),
             reads=[rres["Lb"], carres, cres], writes=[stkres])
        P.op("dve", lambda e: e.tensor_copy(out=carryF[0:2, 0:1], in_=stk[:, 0, 511:512]), reads=[stkres], writes=[carres])
        P.op("dve", lambda e: e.tensor_tensor_scan(out=R["Bneg"], data0=rmask, data1=R["Lc"], initial=0.0,
                                                   op0=ALU.mult, op1=ALU.add),
             reads=[rres["Lc"], cres], writes=[rres["Bneg"]])
        P.op("dve", lambda e: e.tensor_tensor(out=R["d"], in0=R["ig"], in1=R["Bneg"], op=ALU.add),
             reads=[rres["ig"], rres["Bneg"]], writes=[rres["d"]])
        B3 = R["Bneg"].rearrange("p (c l) -> p c l", l=64)
        d3 = R["d"].rearrange("p (c l) -> p c l", l=64)
        a3 = R["a"].rearrange("p (c l) -> p c l", l=64)
        am3 = R["am"].rearrange("p (c l) -> p c l", l=64)
        in3 = R["inter"].rearrange("p (c l) -> p c l", l=64)
        P.op("dve", lambda e: e.tensor_tensor(out=a3, in0=d3, in1=bc(B3[:, :, 63:64], [2, 8, 64]), op=ALU.subtract),
             reads=[rres["d"], rres["Bneg"]], writes=[rres["a"]])
        P.op("dve", lambda e: e.tensor_reduce(out=amax, in_=a3, axis=AX.X, op=ALU.max), reads=[rres["a"]], writes=[smres])
        P.op("dve", lambda e: e.tensor_tensor(out=am3, in0=a3, in1=bc(amax.unsqueeze(2), [2, 8, 64]), op=ALU.subtract),
             reads=[rres["a"], smres], writes=[rres["am"]])
        act_exp(stk[:, 5, :], R["am"], [rres["am"]], [stkres])
        P.op("dve", lambda e: e.tensor_scalar(out=gpos, in0=B3[:, :, 63], scalar1=-1.0, scalar2=None, op0=ALU.mult),
             reads=[rres["Bneg"]], writes=[smres])
        P.op("dve", lambda e: e.tensor_tensor_scan(out=mnew, data0=gpos, data1=amax, initial=mcarry[0:2, 0:1],
                                                   op0=ALU.add, op1=ALU.max), reads=[smres, carres], writes=[smres])
        P.op("dve", lambda e: e.tensor_copy(out=mprev[:, 0:1], in_=mcarry[0:2, 0:1]), reads=[carres], writes=[smres])
        P.op("dve", lambda e: e.tensor_copy(out=mprev[:, 1:8], in_=mnew[:, 0:7]), reads=[smres], writes=[smres])
        P.op("dve", lambda e: e.tensor_copy(out=mcarry[0:2, 0:1], in_=mnew[:, 7:8]), reads=[smres], writes=[carres])
        P.op("dve", lambda e: e.tensor_tensor(out=so16[:, 0:8], in0=gpos, in1=mprev, op=ALU.add), reads=[smres], writes=[smres])
        P.op("dve", lambda e: e.tensor_tensor(out=so16[:, 0:8], in0=so16[:, 0:8], in1=mnew, op=ALU.subtract), reads=[smres], writes=[smres])
        P.op("dve", lambda e: e.tensor_tensor(out=so16[:, 8:16], in0=amax, in1=mnew, op=ALU.subtract), reads=[smres], writes=[smres])
        act_exp(so16, so16, [smres], [smres])
        P.op("dve", lambda e: e.tensor_tensor(out=in3, in0=bc(mprev.unsqueeze(2), [2, 8, 64]), in1=B3, op=ALU.subtract),
             reads=[smres, rres["Bneg"]], writes=[rres["inter"]])
        P.op("dve", lambda e: e.tensor_tensor_scan(out=R["cmx"], data0=negbig, data1=R["d"], initial=-1e30,
                                                   op0=ALU.add, op1=ALU.max), reads=[rres["d"], cres], writes=[rres["cmx"]])
        P.op("dve", lambda e: e.tensor_tensor(out=R["rmx"], in0=R["cmx"], in1=R["Bneg"], op=ALU.subtract),
             reads=[rres["cmx"], rres["Bneg"]], writes=[rres["rmx"]])
        P.op("dve", lambda e: e.tensor_tensor(out=R["mt"], in0=R["inter"], in1=R["rmx"], op=ALU.max),
             reads=[rres["inter"], rres["rmx"]], writes=[rres["mt"]])
        act_exp(stk[:, 1, :], R["d"], [rres["d"]], [stkres])
        P.op("dve", lambda e: e.scalar_tensor_tensor(out=R["t2"], in0=R["Bneg"], scalar=-1.0, in1=R["mt"], op0=ALU.mult, op1=ALU.subtract),
             reads=[rres["Bneg"], rres["mt"]], writes=[rres["t2"]])
        act_exp(stk[:, 2, :], R["t2"], [rres["t2"]], [stkres])
        P.op("dve", lambda e: e.tensor_tensor(out=R["t3"], in0=R["inter"], in1=R["mt"], op=ALU.subtract),
             reads=[rres["inter"], rres["mt"]], writes=[rres["t3"]])
        act_exp(stk[:, 3, :], R["t3"], [rres["t3"]], [stkres])
        act_exp(stk[:, 4, :], R["mt"], [rres["mt"]], [stkres], scale=-1.0)
        TQ, TQres = cx.bank(0, 1)
        for tt in range(4):
            for q in range(6):
                P.op("pe", lambda e, tt=tt, q=q: e.transpose(TQ[:, (tt * 6 + q) * 2:(tt * 6 + q) * 2 + 2],
                                                              stk[:, q, tt * 128:(tt + 1) * 128], identf[0:2, 0:2]),
                     reads=[stkres, cres], writes=[TQres])
        P.op("dve", lambda e: e.tensor_copy(out=tokq, in_=TQ[:, 0:48]), reads=[TQres], writes=[tokqres])
        P.op("dve", lambda e, j=j: e.tensor_copy(out=FnegCol[:, 4 * j:4 * j + 4, :], in_=tokq4[:, :, 0, :]),
             reads=[tokqres], writes=[fcres])
        BB, BBres = cx.bank(1, 0)
        P.op("pe", lambda e: e.matmul(BB[:, 0:16], lhsT=selB, rhs=so16, start=True, stop=True), reads=[smres, cres], writes=[BBres])
        P.op("pe", lambda e: e.matmul(BB[:, 16:17], lhsT=sel0, rhs=stk[:, 0, 511:512], start=True, stop=True),
             reads=[stkres, cres], writes=[BBres])
        P.op("pe", lambda e: e.matmul(BB[:, 17:18], lhsT=sel1, rhs=stk[:, 0, 511:512], start=True, stop=True),
             reads=[stkres, cres], writes=[BBres])
        P.op("dve", lambda e: e.tensor_copy(out=sbc, in_=BB[:, 0:18]), reads=[BBres], writes=[sbcres])
        nkb = 4 * j + 4
        for h in range(2):
            P.op("dve", lambda e, h=h, nkb=nkb: e.tensor_scalar(out=biasf[:, h, 0:nkb], in0=FnegCol[:, 0:nkb, h],
                                                                 scalar1=sbc[:, 16 + h:17 + h], scalar2=None, op0=ALU.subtract),
                 reads=[fcres, sbcres], writes=[biasres])

        if j > 0 and DBG_STAGE >= 2:
            P.op("pool", lambda e: e.tensor_copy(out=akTw[:, 0:128], in_=akTw[:, 512:640]), reads=[akres], writes=[akres])
            P.op("pool", lambda e: e.tensor_copy(out=aVw[:, 0, :], in_=aVw[:, 4, :]), reads=[aVres], writes=[aVres])

        for tt in range(4 if DBG_STAGE >= 2 else 0):
            gt = 4 * j + tt
            QK = cx.banks[1]; QKres = list(cx.bres[1])
            Vp = cx.banks[2]; Vres = list(cx.bres[2])
            gcur[0] = 2.0
            for (dst, w, lo, hi, rs) in ((QK, wqk, 0, 512, QKres[0]), (QK, wqk, 512, 640, QKres[1]),
                                         (Vp, wv, 0, 512, Vres[0]), (Vp, wv, 512, 576, Vres[1])):
                for kc in range(8):
                    P.op("pe", lambda e, kc=kc, dst=dst, w=w, lo=lo, hi=hi, tt=tt, ht=ht: e.matmul(
                        dst[:, lo:hi], lhsT=ht[:, kc, tt * 128:(tt + 1) * 128], rhs=w[:, kc, lo:hi],
                        start=(kc == 0), stop=(kc == 7)), reads=[wres, htres[hs]], writes=[rs])
            gcur[0] = 2.1
            P.op("act", lambda e, QK=QK: e.activation(out=sqb, in_=QK[:, 0:640], func=AF.Square), reads=QKres, writes=[sqres])
            P.op("dve", lambda e: e.tensor_reduce(out=ssq[:, 0:10], in_=sqb.rearrange("p (h d) -> p h d", d=64), axis=AX.X, op=ALU.add),
                 reads=[sqres], writes=[ssqres])
            rstd_chain(ssq[:, 0:10], ssq[:, 0:10], 10, [ssqres], [ssqres])
            P.op("dve", lambda e, QK=QK: e.tensor_tensor(out=qn.rearrange("p (h d) -> p h d", d=64),
                                                         in0=QK[:, 0:640].rearrange("p (h d) -> p h d", d=64),
                                                         in1=bc(ssq[:, 0:10].unsqueeze(2), [128, 10, 64]), op=ALU.mult),
                 reads=QKres + [ssqres], writes=[qnres])
            P.op("pool", lambda e: e.tensor_tensor(out=qn[:, 0:384], in0=qn[:, 0:384], in1=gqk[:, 0:384], op=ALU.mult),
                 reads=[qnres, cres], writes=[qnres])
            P.op("dve", lambda e: e.tensor_tensor(out=qkb[:, 384:640], in0=qn[:, 384:640], in1=gqk[:, 384:640], op=ALU.mult),
                 reads=[qnres, cres], writes=[qkbres])
            gcur[0] = 2.2
            x4 = qn[:, 0:384].rearrange("p (a f) -> p a f", f=32)
            rA4 = rA.rearrange("p (a f) -> p a f", f=32)
            rB4 = rB.rearrange("p (a f) -> p a f", f=32)
            P.op("dve", lambda e, gt=gt: e.tensor_tensor(out=rA4, in0=x4, in1=bc(cosT[:, gt, :].unsqueeze(1), [128, 12, 32]), op=ALU.mult),
                 reads=[qnres, cres], writes=[rres2])
            P.op("pool", lambda e, gt=gt: e.tensor_tensor(out=rB4, in0=x4, in1=bc(sinT[:, gt, :].unsqueeze(1), [128, 12, 32]), op=ALU.mult),
                 reads=[qnres, cres], writes=[rres2])
            rA5 = rA.rearrange("p (h t f) -> p h t f", t=2, f=32)
            rB5 = rB.rearrange("p (h t f) -> p h t f", t=2, f=32)
            qk5 = qkb[:, 0:384].rearrange("p (h t f) -> p h t f", t=2, f=32)
            P.op("dve", lambda e: e.tensor_tensor(out=qk5[:, :, 0, :], in0=rA5[:, :, 0, :], in1=rB5[:, :, 1, :], op=ALU.subtract),
                 reads=[rres2], writes=[qkbres])
            P.op("dve", lambda e: e.tensor_tensor(out=qk5[:, :, 1, :], in0=rA5[:, :, 1, :], in1=rB5[:, :, 0, :], op=ALU.add),
                 reads=[rres2], writes=[qkbres])
            gcur[0] = 2.3
            TP, TPres = cx.bank_bf(3, 0)
            for blk in range(5):
                P.op("pe", lambda e, blk=blk: e.transpose(TP[:, blk * 128:(blk + 1) * 128], qkb[:, blk * 128:(blk + 1) * 128], identb),
                     reads=[qkbres, cres], writes=[TPres])
            P.op("act", lambda e, tt=tt: e.activation(out=aqT[:, :, tt * 128:(tt + 1) * 128],
                                                      in_=TP[:, 0:256].rearrange("p (a n) -> p a n", a=2), func=AF.Copy),
                 reads=[TPres], writes=[aqres])
            P.op("act", lambda e, tt=tt: e.activation(out=akTw[:, 128 + tt * 128:256 + tt * 128], in_=TP[:, 256:384], func=AF.Copy),
                 reads=[TPres], writes=[akres])
            P.op("act", lambda e, tt=tt: e.activation(out=bqT[:, tt * 128:(tt + 1) * 128], in_=TP[:, 384:512], func=AF.Copy),
                 reads=[TPres], writes=[bqres])
            P.op("act", lambda e, gt=gt: e.activation(out=bkT[:, gt * 128:(gt + 1) * 128], in_=TP[:, 512:640], func=AF.Copy),
                 reads=[TPres], writes=[bkres[gt]])
            gcur[0] = 2.4
            P.op("act", lambda e, tt=tt, Vp=Vp: e.activation(out=aVw[:, 1 + tt, 0:64], in_=Vp[:, 0:64], func=AF.Copy),
                 reads=Vres, writes=[aVres])
            P.op("act", lambda e, gt=gt, Vp=Vp: e.activation(out=bV[:, gt, :, 0:64], in_=Vp[:, 64:192].rearrange("p (h d) -> p h d", d=64),
                                                             func=AF.Copy), reads=Vres, writes=[bVres[gt]])
            P.op("act", lambda e, tt=tt, Vp=Vp: e.activation(out=cV1[:, tt, :, 0:64], in_=Vp[:, 192:320].rearrange("p (h d) -> p h d", d=64),
                                                             func=AF.Copy), reads=Vres, writes=[vpres[tt]])
            P.op("dve", lambda e, tt=tt, Vp=Vp: e.tensor_tensor(out=cVs[:, tt, :, 0:64], in0=Vp[:, 192:320].rearrange("p (h d) -> p h d", d=64),
                                                                in1=bc(tokq4[:, tt, 1, :].unsqueeze(2), [128, 2, 64]), op=ALU.mult),
                 reads=Vres + [tokqres], writes=[vpres[tt]])
            P.op("dve", lambda e, tt=tt: e.tensor_copy(out=cVs[:, tt, :, 64], in_=tokq4[:, tt, 1, :]), reads=[tokqres], writes=[vpres[tt]])
            P.op("dve", lambda e, tt=tt, Vp=Vp: e.scalar_tensor_tensor(out=kw[:, tt, :, :], in0=Vp[:, 320:448].rearrange("p (h d) -> p h d", d=64),
                                                                       scalar=0.125, in1=bc(tokq4[:, tt, 5, :].unsqueeze(2), [128, 2, 64]),
                                                                       op0=ALU.mult, op1=ALU.mult),
                 reads=Vres + [tokqres], writes=[vpres[tt]])
            gcur[0] = 2.45
            act_exp(eo, Vp[:, 448:576], Vres, [eores], scale=-1.0)
            P.op("dve", lambda e: e.tensor_scalar(out=eo, in0=eo, scalar1=1.0, scalar2=None, op0=ALU.add), reads=[eores], writes=[eores])
            P.op("dve", lambda e, tt=tt: e.reciprocal(out=og[:, tt, :], in_=eo), reads=[eores], writes=[vpres[tt]])

        gcur[0] = 2.5
        for (dstT, lo, rs, bi) in ((cqT, 0, cqres, 0), (ckT, 128, ckres, 1)) if DBG_STAGE >= 2 else ():
            CB, CBres = cx.bank(0, bi)
            for kc in range(8):
                P.op("pe", lambda e, kc=kc, lo=lo, CB=CB, ht=ht: e.matmul(CB, lhsT=wc[:, kc, lo:lo + 128], rhs=ht[:, kc, :],
                                                                          start=(kc == 0), stop=(kc == 7)),
                     reads=[wres, htres[hs]], writes=[CBres])
            P.op("act", lambda e, dstT=dstT, CB=CB: e.activation(out=dstT, in_=CB, func=AF.Copy), reads=[CBres], writes=[rs])

        gcur[0] = 3.0
        for tt in range(4 if DBG_STAGE >= 3 else 0):
            gt = 4 * j + tt
            SP = cx.banks[1]; SPres = list(cx.bres[1])
            SP5 = SP.rearrange("p (r k i n) -> p r k i n", r=2, k=2, i=2)
            gcur[0] = 3.0
            ks = (1,) if gt == 0 else (0, 1)
            for k in ks:
                kc0 = (tt + k) * 128
                for r in range(2):
                    for i in range(2):
                        P.op("pe", lambda e, k=k, r=r, i=i, kc0=kc0, tt=tt, SP5=SP5: e.matmul(
                            SP5[:, r, k, i, :], lhsT=akTw[64 * r:64 * r + 64, kc0:kc0 + 128],
                            rhs=aqT[64 * r:64 * r + 64, i, tt * 128:(tt + 1) * 128], start=True, stop=True),
                            reads=[akres, aqres], writes=[SPres[r]])
            gcur[0] = 3.1
            P.op("act", lambda e, SP=SP: e.activation(out=PTa, in_=SP[:, 0:1024], func=AF.Exp, scale=0.125),
                 reads=SPres, writes=[ptares])
            PT5 = PTa.rearrange("p (r k i n) -> p r k i n", r=2, k=2, i=2)
            gcur[0] = 3.2
            if gt > 0:
                P.op("pool", lambda e, PT5=PT5: e.memset(PT5[0:64, :, 0, :, 64:128], 0.0), reads=[ptares], writes=[ptares])
            P.op("pool", lambda e, PT5=PT5: e.memset(PT5[64:128, :, 1, :, 0:64], 0.0), reads=[ptares], writes=[ptares])
            gcur[0] = 3.3
            OA, OAres = cx.bank(3, 1)
            for hd in range(4):
                for k in ks:
                    P.op("pe", lambda e, hd=hd, k=k, tt=tt, OA=OA, ks=ks, PT5=PT5: e.matmul(
                        OA[:, hd * 65:hd * 65 + 65], lhsT=PT5[:, hd // 2, k, hd % 2, :], rhs=aVw[:, tt + k, :],
                        start=(k == ks[0]), stop=(k == 1)), reads=[ptares, aVres], writes=[OAres])
            gcur[0] = 3.4
            OA3 = OA[:, 0:260].rearrange("p (h c) -> p h c", c=65)
            P.op("dve", lambda e, OA3=OA3: e.tensor_tensor(out=sm4[:, 0:4], in0=OA3[:, :, 64], in1=esk, op=ALU.add),
                 reads=[OAres, cres], writes=[sm4res])
            P.op("dve", lambda e: e.reciprocal(out=sm4[:, 0:4], in_=sm4[:, 0:4]), reads=[sm4res], writes=[sm4res])
            ydst = ytok[:, tt, 0:256].rearrange("p (i r d) -> p r i d", i=2, r=2)
            OA4 = OA[:, 0:260].rearrange("p (r i c) -> p r i c", r=2, i=2)
            P.op("dve", lambda e, ydst=ydst, OA4=OA4: e.tensor_tensor(
                out=ydst, in0=OA4[:, :, :, 0:64],
                in1=bc(sm4[:, 0:4].rearrange("p (r i) -> p r i", r=2).unsqueeze(3), [128, 2, 2, 64]), op=ALU.mult),
                reads=[OAres, sm4res], writes=[ytres[tt]])

        gcur[0] = 4.0
        for h in range(2 if DBG_STAGE >= 4 else 0):
            OF, OFres = cx.bank(2, h)
            for kb in range(nkb):
                nq0 = max(0, kb - 4 * j)
                N = 512 - 128 * nq0
                SF, SFres = cx.bank(3 if kb % 2 == 0 else 1, h)
                ptf = PTf[kb % 2]
                P.op("pe", lambda e, kb=kb, h=h, nq0=nq0, N=N, SF=SF: e.matmul(
                    SF[:, 0:N], lhsT=bkT[64 * h:64 * h + 64, kb * 128:(kb + 1) * 128],
                    rhs=bqT[64 * h:64 * h + 64, nq0 * 128:512], start=True, stop=True),
                    reads=[bkres[kb], bqres], writes=[SFres])
                act_exp(ptf[:, 0:N], SF[:, 0:N], [SFres, biasres], [ptfres[kb % 2]], scale=0.125, bias=biasf[:, h, kb:kb + 1])
                if kb >= 4 * j:
                    P.op("pool", lambda e, ptf=ptf: e.tensor_tensor(out=ptf[:, 0:128], in0=ptf[:, 0:128], in1=tri, op=ALU.mult),
                         reads=[ptfres[kb % 2], cres], writes=[ptfres[kb % 2]])
                for qt in range(nq0, 4):
                    P.op("pe", lambda e, qt=qt, nq0=nq0, kb=kb, h=h, OF=OF, ptf=ptf, j=j: e.matmul(
                        OF[:, qt * 65:qt * 65 + 65], lhsT=ptf[:, (qt - nq0) * 128:(qt - nq0 + 1) * 128],
                        rhs=bV[:, kb, h, :], start=(kb == 0 and qt == 0), stop=(kb == 4 * j + qt)),
                        reads=[ptfres[kb % 2], bVres[kb]], writes=[OFres])
            OF3 = OF[:, 0:260].rearrange("p (q c) -> p q c", c=65)
            P.op("dve", lambda e, OF3=OF3, h=h: e.reciprocal(out=sm4[:, 4 + 4 * h:8 + 4 * h], in_=OF3[:, :, 64]),
                 reads=[OFres], writes=[sm4res])
            P.op("dve", lambda e, OF3=OF3, h=h: e.tensor_tensor(
                out=ytok[:, :, 256 + 64 * h:320 + 64 * h], in0=OF3[:, :, 0:64],
                in1=bc(sm4[:, 4 + 4 * h:8 + 4 * h].unsqueeze(2), [128, 4, 64]), op=ALU.mult),
                reads=[OFres, sm4res], writes=ytres)

        gcur[0] = 5.0
        for tt in range(4 if DBG_STAGE >= 5 else 0):
            X, Xres = cx.bank(0, 0)
            CLb = [cx.bank(0, 1), cx.bank(2, 1)]
            X1b = [cx.bank(3, 0), cx.bank(3, 1)]
            for h in range(2):
                SR, SRres = cx.bank(1, h)
                P.op("pe", lambda e, h=h, tt=tt, SR=SR: e.matmul(
                    SR[:, 0:128], lhsT=ckT[64 * h:64 * h + 64, tt * 128:(tt + 1) * 128],
                    rhs=cqT[64 * h:64 * h + 64, tt * 128:(tt + 1) * 128], start=True, stop=True),
                    reads=[ckres, cqres], writes=[SRres])
                P.op("dve", lambda e, h=h, SR=SR: e.tensor_tensor(out=AT[h], in0=SR[:, 0:128], in1=m2, op=ALU.mult),
                     reads=[SRres, cres], writes=[atres[h]])
                P.op("pe", lambda e, h=h, tt=tt, X=X: e.matmul(X[:, h * 65:h * 65 + 65], lhsT=AT[h], rhs=cVs[:, tt, h, :],
                                                              start=True, stop=True),
                     reads=[atres[h], vpres[tt]], writes=[Xres])
                for c2_ in range(2):
                    P.op("pe", lambda e, h=h, tt=tt, c2_=c2_, CL=CLb[c2_][0]: e.matmul(
                        CL[64 * h:64 * h + 64, 0:65], lhsT=kw[64 * c2_:64 * c2_ + 64, tt, h, :],
                        rhs=cV1[64 * c2_:64 * c2_ + 64, tt, h, :], start=True, stop=True),
                        reads=[vpres[tt]], writes=[CLb[c2_][1]])
            for c2_ in range(2):
                cc = 2 * tt + c2_
                cur = Cbf[cbi]
                for h in range(2):
                    P.op("pe", lambda e, h=h, tt=tt, c2_=c2_, X1=X1b[h][0], cur=cur: e.matmul(
                        X1[64 * c2_:64 * c2_ + 64, 0:65],
                        lhsT=cqT[64 * h:64 * h + 64, tt * 128 + 64 * c2_:tt * 128 + 64 * c2_ + 64],
                        rhs=cur[64 * h:64 * h + 64, :], start=True, stop=True),
                        reads=[cqres, cbfres[cbi]], writes=[X1b[h][1]])
                P.op("dve", lambda e, c2_=c2_, cc=cc, CL=CLb[c2_][0]: e.tensor_scalar(out=tmpc, in0=CL[:, 0:65],
                                                                           scalar1=sbc[:, 8 + cc:9 + cc], scalar2=None, op0=ALU.mult),
                     reads=[CLb[c2_][1], sbcres], writes=[tmpcres])
                P.op("dve", lambda e, cc=cc: e.scalar_tensor_tensor(out=Cst, in0=Cst, scalar=sbc[:, cc:cc + 1], in1=tmpc,
                                                                    op0=ALU.mult, op1=ALU.add),
                     reads=[cstres, tmpcres, sbcres], writes=[cstres])
                cbi = 1 - cbi
                P.op("dve", lambda e, nxt=Cbf[cbi]: e.tensor_copy(out=nxt, in_=Cst), reads=[cstres], writes=[cbfres[cbi]])
            X2 = X[:, 0:130].rearrange("p (h c) -> p h c", c=65)
            for h in range(2):
                P.op("dve", lambda e, tt=tt, h=h, X1=X1b[h][0]: e.tensor_scalar(out=t1[:, h, :], in0=X1[:, 0:65],
                                                                             scalar1=tokq4[:, tt, 3, h:h + 1], scalar2=None, op0=ALU.mult),
                     reads=[X1b[h][1], tokqres], writes=[hnres])
            P.op("dve", lambda e, tt=tt, X2=X2: e.tensor_tensor(out=hn, in0=X2, in1=bc(tokq4[:, tt, 2, :].unsqueeze(2), [128, 2, 65]), op=ALU.mult),
                 reads=[Xres, tokqres], writes=[hnres])
            P.op("dve", lambda e: e.tensor_tensor(out=hn, in0=hn, in1=t1, op=ALU.add), reads=[hnres], writes=[hnres])
            P.op("dve", lambda e: e.scalar_tensor_tensor(out=sm2[:, 0:2], in0=hn[:, :, 64], scalar=-1.0, in1=hn[:, :, 64], op0=ALU.mult, op1=ALU.max),
                 reads=[hnres], writes=[sm2res])
            P.op("dve", lambda e, tt=tt: e.tensor_tensor(out=sm2[:, 0:2], in0=sm2[:, 0:2], in1=tokq4[:, tt, 4, :], op=ALU.max),
                 reads=[sm2res, tokqres], writes=[sm2res])
            P.op("dve", lambda e: e.reciprocal(out=sm2[:, 0:2], in_=sm2[:, 0:2]), reads=[sm2res], writes=[sm2res])
            P.op("dve", lambda e: e.tensor_tensor(out=ht_, in0=hn[:, :, 0:64], in1=bc(sm2[:, 0:2].unsqueeze(2), [128, 2, 64]), op=ALU.mult),
                 reads=[hnres, sm2res], writes=[htres2])
            P.op("pool", lambda e: e.tensor_tensor(out=sqh, in0=ht_, in1=ht_, op=ALU.mult), reads=[htres2], writes=[htres2])
            P.op("dve", lambda e: e.tensor_reduce(out=sm2[:, 2:4], in_=sqh, axis=AX.X, op=ALU.add), reads=[htres2], writes=[sm2res])
            rstd_chain(sm2[:, 2:4], sm2[:, 2:4], 2, [sm2res], [sm2res])
            P.op("dve", lambda e: e.tensor_tensor(out=ht_, in0=ht_, in1=bc(sm2[:, 2:4].unsqueeze(2), [128, 2, 64]), op=ALU.mult),
                 reads=[htres2, sm2res], writes=[htres2])
            P.op("pool", lambda e: e.tensor_tensor(out=ht_, in0=ht_, in1=gc.rearrange("p (h d) -> p h d", d=64), op=ALU.mult),
                 reads=[htres2, cres], writes=[htres2])
            P.op("dve", lambda e, tt=tt: e.tensor_tensor(out=ytok[:, tt, 384:512].rearrange("p (h d) -> p h d", d=64), in0=ht_,
                                                         in1=og[:, tt, :].rearrange("p (h d) -> p h d", d=64), op=ALU.mult),
                 reads=[htres2, vpres[tt]], writes=[ytres[tt]])

        gcur[0] = 0.0
        for tt in range(4):
            YP, YPres = cx.bank_bf(2, tt % 2)
            for fc in range(4):
                P.op("pe", lambda e, tt=tt, fc=fc, YP=YP: e.transpose(YP[:, fc * 128:(fc + 1) * 128], ytok[:, tt, fc * 128:(fc + 1) * 128], identb),
                     reads=[ytres[tt], cres], writes=[YPres])
            P.op("act", lambda e, tt=tt, YP=YP: e.activation(out=yTst[:, :, tt * 128:(tt + 1) * 128],
                                                             in_=YP[:, 0:512].rearrange("p (f n) -> p f n", f=4), func=AF.Copy),
                 reads=[YPres], writes=[ystres])
        for fc in range(4):
            P.dma("sp", lambda e, fc=fc, j=j: e.dma_start(out=yT_d[fc * 128:(fc + 1) * 128, j * 512:(j + 1) * 512], in_=yTst[:, fc, :]),
                  "st_y", reads=[ystres], final=True)
    return cx.finish()


def consts_f():
    cf = np.zeros((128, 2304), np.float32)
    cf[:, 0:128] = np.eye(128)
    s = np.arange(128)[:, None]
    l = np.arange(128)[None, :]
    cf[:, 128:256] = np.where((s // 64 == l // 64) & (s <= l), 0.125, 0.0)
    inv = (10000.0 ** (-np.arange(32, dtype=np.float32) / 32)).astype(np.float32)
    pos = (np.arange(NT)[None, :, None] * 128 + np.arange(128)[:, None, None]).astype(np.float32)
    ang = (pos * inv[None, None, :]).astype(np.float32)
    cf[:, 256:1280] = np.cos(ang).reshape(128, 1024)
    cf[:, 1280:2304] = np.sin(ang).reshape(128, 1024)
    return cf


def consts_2():
    c2 = np.zeros((2, 1920), np.float32)
    c2[0, 0:64] = 1.0; c2[1, 64:128] = 1.0
    c2[0, 128:256] = 1.0
    c2[1, 256:384] = 1.0
    starts = (np.arange(512) % 64 == 0)
    c2[:, 384:896] = np.where(starts, 0.0, 1.0)
    c2[:, 896:1408] = np.where(starts, -1e30, 0.0)
    c2[:, 1408:1920] = 1.0
    return c2


def m_inputs(p, l, e):
    w = p["w_in"][l]
    sl = lambda a, n: np.arange(a, a + n)
    aq = sl(256 * e, 256); ak = sl(512 + 64 * e, 64); av = sl(640 + 64 * e, 64)
    bq = sl(768 + 128 * e, 128); bk = sl(1024 + 128 * e, 128); bv = sl(1280 + 128 * e, 128); bfc = sl(1536 + 2 * e, 2)
    cq = sl(1540 + 128 * e, 128); ck = sl(1796 + 128 * e, 128); cv = sl(2052 + 128 * e, 128)
    ci = sl(2308 + 2 * e, 2); cfc = sl(2312 + 2 * e, 2); co = sl(2316 + 128 * e, 128)
    wqk = w[:, np.concatenate([aq, ak, ak, bq, bk])]
    wv = w[:, np.concatenate([av, bv, cv, ck, co])]
    wc = w[:, np.concatenate([cq, ck])]
    wg = np.zeros((D, 128), np.float32)
    wg[:, 0:2] = w[:, bfc]; wg[:, 32:34] = w[:, ci]; wg[:, 64:66] = w[:, cfc]
    gqk = np.concatenate([np.tile(p["swa_q_norm"][l], 4), np.tile(p["swa_k_norm"][l], 2),
                          np.tile(p["fox_q_norm"][l], 2), np.tile(p["fox_k_norm"][l], 2)])
    gc = p["mlstm_out_norm"][l, 2 * e:2 * e + 2].reshape(128)
    sk = p["swa_sinks"][l, 4 * e + np.array([0, 2, 1, 3])]
    pv = np.zeros((2, 4), np.float32)
    pv[:, 0] = p["fox_f_bias"][l, 2 * e:2 * e + 2]
    pv[:, 1] = p["mlstm_i_bias"][l, 2 * e:2 * e + 2]
    pv[:, 2] = p["mlstm_f_bias"][l, 2 * e:2 * e + 2]
    c = np.ascontiguousarray
    return {"wqk": c(wqk), "wv": c(wv), "wc": c(wc), "wg": wg, "gqk": c(np.tile(gqk, (128, 1))),
            "gc": c(np.tile(gc, (128, 1))), "sk": c(np.tile(sk, (128, 1))), "pv": pv,
            "cf": consts_f(), "c2": consts_2(), "cb": consts_bf()}


def y_feature_index(e):
    return np.concatenate([np.arange(256 * e, 256 * e + 256), 512 + 128 * e + np.arange(128),
                           768 + 128 * e + np.arange(128)])


def _run8(nc, in_maps):
    res = run_bass_kernel_spmd(nc, in_maps, core_ids=list(range(8)))
    return res.results


def kernel(**inputs):
    p = {k: np.asarray(v) for k, v in inputs.items()}
    x = np.ascontiguousarray(p["x"], dtype=np.float32)
    c = np.ascontiguousarray
    cb = consts_bf()
    tile128 = lambda v: c(np.tile(np.asarray(v, np.float32), (128, 1)))
    be = [(cidx // 2, cidx % 2) for cidx in range(8)]

    maps = [{"x": c(x[b, HT * e:HT * (e + 1)]), "g": tile128(p["norm1"][0]), "cb": cb} for (b, e) in be]
    r = _run8(build_p0(), maps)
    hT = [np.concatenate([np.asarray(r[2 * b]["hT"]), np.asarray(r[2 * b + 1]["hT"])], axis=1) for b in range(4)]
    xcur = x
    for l in range(2):
        last = (l == 1)
        mi = [m_inputs(p, l, e) for e in range(2)]
        maps = []
        for (b, e) in be:
            d = dict(mi[e])
            d["hT"] = c(hT[b])
            maps.append(d)
        r = _run8(build_m(), maps)
        yT = []
        for b in range(4):
            full = np.zeros((D, S), dtype=ml_dtypes.bfloat16)
            for e in range(2):
                full[y_feature_index(e)] = np.asarray(r[2 * b + e]["yT"])
            yT.append(full)
        maps = []
        for (b, e) in be:
            d = {"x": c(xcur[b, HT * e:HT * (e + 1)]), "yT": c(yT[b][:, HT * e:HT * (e + 1)]),
                 "w_out": c(p["w_out"][l]), "w_ff1": c(p["w_ff1"][l]), "w_ff2": c(p["w_ff2"][l]),
                 "g2": tile128(p["norm2"][l]), "cb": cb}
            if not last:
                d["gn"] = tile128(p["norm1"][l + 1])
            maps.append(d)
        r = _run8(build_f(last), maps)
        xn = np.zeros((4, S, D), np.float32)
        for cidx, (b, e) in enumerate(be):
            xn[b, HT * e:HT * (e + 1)] = np.asarray(r[cidx]["xo"])
        xcur = xn
        if not last:
            hT = [np.concatenate([np.asarray(r[2 * b]["hT"]), np.asarray(r[2 * b + 1]["hT"])], axis=1) for b in range(4)]
    return xcur
```
